# Optimizing a Trainium2 kernel written in Bass

```python
import math
import jax, jax.numpy as jnp
from jax import lax
import numpy as np


D_MODEL = 1024
BATCH = 8
SEQ = 8192
DEPTH = 4
DEC_BATCH = 32
DEC_SEQ = 16
PAST_LEN = 2048

CHUNK = 64
HEAD_DIM = 64
N_HEADS_A = 8
BAND_CHUNKS = 8
BAND_PAST = BAND_CHUNKS * CHUNK
MAX_REL = 128
N_HEADS_B = 4
WIDTH_A = N_HEADS_A * HEAD_DIM
WIDTH_B = N_HEADS_B * 2 * HEAD_DIM
ATTN_IN = 3 * WIDTH_A + 3 * WIDTH_B
ATTN_OUT = WIDTH_A + WIDTH_B
CONV_WIDTH = 3
D_FF = -(-8 * D_MODEL // (3 * 256)) * 256
ROPE_THETA = 10000.0
EPS = 1e-6
Q_BLOCK = 128
NEG = -1e30
N_ATTN_LAYERS = (DEPTH + 1) // 2
N_CONV_LAYERS = DEPTH // 2

kernel_name = "chunk_stream_hybrid_band_diff_conv"


def rms_norm(x, g):
    xf = x.astype(jnp.float32)
    y = xf * lax.rsqrt(jnp.mean(xf * xf, axis=-1, keepdims=True) + EPS)
    return (y * g.astype(jnp.float32)).astype(x.dtype)


def rotary(x, pos):
    half = x.shape[-1] // 2
    inv = ROPE_THETA ** (-jnp.arange(half, dtype=jnp.float32) / half)
    ang = pos.astype(jnp.float32)[:, None] * inv[None, :]
    shape = (ang.shape[0],) + (1,) * (x.ndim - 3) + (half,)
    c = jnp.cos(ang).reshape(shape)
    s = jnp.sin(ang).reshape(shape)
    xf = x.astype(jnp.float32)
    x1, x2 = xf[..., :half], xf[..., half:]
    return jnp.concatenate([x1 * c - x2 * s, x2 * c + x1 * s], axis=-1).astype(x.dtype)


def attn_project(h, w_in):
    b, s, _ = h.shape
    z = h @ w_in
    cuts = [WIDTH_A, 2 * WIDTH_A, 3 * WIDTH_A, 3 * WIDTH_A + WIDTH_B, 3 * WIDTH_A + 2 * WIDTH_B]
    qa, ka, va, qb, kb, vb = jnp.split(z, cuts, axis=-1)
    qa = qa.reshape(b, s, N_HEADS_A, HEAD_DIM)
    ka = ka.reshape(b, s, N_HEADS_A, HEAD_DIM)
    va = va.reshape(b, s, N_HEADS_A, HEAD_DIM)
    qb = qb.reshape(b, s, N_HEADS_B, 2, HEAD_DIM)
    kb = kb.reshape(b, s, N_HEADS_B, 2, HEAD_DIM)
    vb = vb.reshape(b, s, N_HEADS_B, 2 * HEAD_DIM)
    return qa, ka, va, qb, kb, vb


def rel_bias_block(table, q_offset, n_q, n_k):
    rel = q_offset + np.arange(n_q)[:, None] - np.arange(n_k)[None, :]
    idx = np.clip(rel, -MAX_REL, MAX_REL) + MAX_REL
    return table[:, idx].astype(jnp.float32)


def band_attn(q, k, v, bias, valid):
    s = jnp.einsum('bqhd,bkhd->bhqk', q, k).astype(jnp.float32) / math.sqrt(HEAD_DIM) + bias
    if valid is not None:
        s = jnp.where(valid, s, NEG)
    p = jax.nn.softmax(s, axis=-1).astype(v.dtype)
    return jnp.einsum('bhqk,bkhd->bqhd', p, v)


def diff_lambda(lq1, lk1, lq2, lk2, lam_init):
    f = jnp.float32
    return (jnp.exp(jnp.sum(lq1.astype(f) * lk1.astype(f)))
            - jnp.exp(jnp.sum(lq2.astype(f) * lk2.astype(f))) + lam_init)


def diff_attn(q, k, v, valid, lam, subln_g, lam_init):
    s = jnp.einsum('bqhmd,bkhmd->bhmqk', q, k).astype(jnp.float32) / math.sqrt(HEAD_DIM)
    if valid is not None:
        s = jnp.where(valid, s, NEG)
    p = jax.nn.softmax(s, axis=-1)
    a = (p[:, :, 0] - lam * p[:, :, 1]).astype(v.dtype)
    o = jnp.einsum('bhqk,bkhe->bqhe', a, v)
    return rms_norm(o, subln_g) * (1.0 - lam_init)


def attn_mixer_prompt(h, w_in, w_o, table, lam, subln_g, lam_init):
    b, s, _ = h.shape
    qa, ka, va, qb, kb, vb = attn_project(h, w_in)
    pos = jnp.arange(s)
    qb = rotary(qb, pos)
    kb = rotary(kb, pos)
    band = BAND_PAST + CHUNK
    kpad = jnp.pad(ka, ((0, 0), (BAND_PAST, 0), (0, 0), (0, 0)))
    vpad = jnp.pad(va, ((0, 0), (BAND_PAST, 0), (0, 0), (0, 0)))
    bias = rel_bias_block(table, BAND_PAST, CHUNK, band)

    def a_chunk(c):
        start = c * CHUNK
        qc = lax.dynamic_slice_in_dim(qa, start, CHUNK, axis=1)
        kc = lax.dynamic_slice_in_dim(kpad, start, band, axis=1)
        vc = lax.dynamic_slice_in_dim(vpad, start, band, axis=1)
        valid = (start - BAND_PAST + jnp.arange(band)) >= 0
        return band_attn(qc, kc, vc, bias, valid)

    oa = lax.map(a_chunk, jnp.arange(s // CHUNK))
    oa = jnp.moveaxis(oa, 0, 1).reshape(b, s, WIDTH_A)
    key_chunk = jnp.arange(s) // CHUNK

    def b_block(i):
        start = i * Q_BLOCK
        qblk = lax.dynamic_slice_in_dim(qb, start, Q_BLOCK, axis=1)
        q_chunk = (start + jnp.arange(Q_BLOCK)) // CHUNK
        valid = key_chunk[None, :] <= q_chunk[:, None]
        return diff_attn(qblk, kb, vb, valid, lam, subln_g, lam_init)

    ob = lax.map(b_block, jnp.arange(s // Q_BLOCK))
    ob = jnp.moveaxis(ob, 0, 1).reshape(b, s, WIDTH_B)
    y = jnp.concatenate([oa, ob], axis=-1) @ w_o
    keep = min(BAND_PAST, s)
    return y, ka[:, s - keep:], va[:, s - keep:], kb, vb


def attn_mixer_sample(h, ka_c, va_c, kb_c, vb_c, w_in, w_o, table, lam, subln_g, lam_init):
    b, t, _ = h.shape
    past = kb_c.shape[1]
    a_past = ka_c.shape[1]
    qa, ka, va, qb, kb, vb = attn_project(h, w_in)
    pos = past + jnp.arange(t)
    qb = rotary(qb, pos)
    kb = rotary(kb, pos)
    k_all = jnp.concatenate([ka_c, ka], axis=1)
    v_all = jnp.concatenate([va_c, va], axis=1)
    bias = rel_bias_block(table, a_past, t, a_past + t)
    oa = band_attn(qa, k_all, v_all, bias, None).reshape(b, t, WIDTH_A)
    kb_all = jnp.concatenate([kb_c, kb], axis=1)
    vb_all = jnp.concatenate([vb_c, vb], axis=1)
    ob = diff_attn(qb, kb_all, vb_all, None, lam, subln_g, lam_init).reshape(b, t, WIDTH_B)
    y = jnp.concatenate([oa, ob], axis=-1) @ w_o
    return y, ka, va, kb, vb


def conv_mixer(h, w_in, conv_w, w_out, conv_state):
    b, s, d = h.shape
    gate_b, gate_c, u = jnp.split(h @ w_in, 3, axis=-1)
    xin = gate_c * u
    if conv_state is None:
        conv_state = jnp.zeros((b, CONV_WIDTH - 1, d), xin.dtype)
    xpad = jnp.concatenate([conv_state.astype(xin.dtype), xin], axis=1)
    conv = lax.conv_general_dilated(xpad, conv_w[:, None, :].astype(xin.dtype), window_strides=(1,),
                                    padding='VALID', dimension_numbers=('NWC', 'WIO', 'NWC'),
                                    feature_group_count=d)
    y = (gate_b * conv) @ w_out
    return y, xpad[:, -(CONV_WIDTH - 1):]


def swiglu(h, wg, wu, wd):
    return (jax.nn.silu(h @ wg) * (h @ wu)) @ wd


def setup_inputs(seed: int = 0) -> dict:
    key = jax.random.key(seed)
    ks = jax.random.split(key, 32)
    f = jnp.float32
    a_past = min(BAND_PAST, PAST_LEN)
    nrm = lambda k, shape, scale: jax.random.normal(k, shape, f) * scale
    return {
        'x_prompt': nrm(ks[0], (BATCH, SEQ, D_MODEL), 1.0),
        'x_sample': nrm(ks[1], (DEC_BATCH, DEC_SEQ, D_MODEL), 1.0),
        'cache_a_k': nrm(ks[2], (N_ATTN_LAYERS, DEC_BATCH, a_past, N_HEADS_A, HEAD_DIM), 1.0),
        'cache_a_v': nrm(ks[3], (N_ATTN_LAYERS, DEC_BATCH, a_past, N_HEADS_A, HEAD_DIM), 1.0),
        'cache_b_k': nrm(ks[4], (N_ATTN_LAYERS, DEC_BATCH, PAST_LEN, N_HEADS_B, 2, HEAD_DIM), 1.0),
        'cache_b_v': nrm(ks[5], (N_ATTN_LAYERS, DEC_BATCH, PAST_LEN, N_HEADS_B, 2 * HEAD_DIM), 1.0),
        'state_conv': nrm(ks[6], (N_CONV_LAYERS, DEC_BATCH, CONV_WIDTH - 1, D_MODEL), 1.0),
        'norm_mix': 1.0 + nrm(ks[7], (DEPTH, D_MODEL), 0.02),
        'norm_ffn': 1.0 + nrm(ks[8], (DEPTH, D_MODEL), 0.02),
        'norm_final': 1.0 + nrm(ks[9], (D_MODEL,), 0.02),
        'w_attn_in': nrm(ks[10], (N_ATTN_LAYERS, D_MODEL, ATTN_IN), D_MODEL ** -0.5),
        'w_attn_out': nrm(ks[11], (N_ATTN_LAYERS, ATTN_OUT, D_MODEL), ATTN_OUT ** -0.5),
        'rel_bias': nrm(ks[12], (N_ATTN_LAYERS, N_HEADS_A, 2 * MAX_REL + 1), 0.1),
        'lambda_q1': nrm(ks[13], (N_ATTN_LAYERS, HEAD_DIM), 0.1),
        'lambda_k1': nrm(ks[14], (N_ATTN_LAYERS, HEAD_DIM), 0.1),
        'lambda_q2': nrm(ks[15], (N_ATTN_LAYERS, HEAD_DIM), 0.1),
        'lambda_k2': nrm(ks[16], (N_ATTN_LAYERS, HEAD_DIM), 0.1),
        'subln_g': 1.0 + nrm(ks[17], (N_ATTN_LAYERS, 2 * HEAD_DIM), 0.02),
        'w_conv_in': nrm(ks[18], (N_CONV_LAYERS, D_MODEL, 3 * D_MODEL), D_MODEL ** -0.5),
        'conv_w': nrm(ks[19], (N_CONV_LAYERS, CONV_WIDTH, D_MODEL), CONV_WIDTH ** -0.5),
        'w_conv_out': nrm(ks[20], (N_CONV_LAYERS, D_MODEL, D_MODEL), D_MODEL ** -0.5),
        'w_ffn_gate': nrm(ks[21], (DEPTH, D_MODEL, D_FF), D_MODEL ** -0.5),
        'w_ffn_up': nrm(ks[22], (DEPTH, D_MODEL, D_FF), D_MODEL ** -0.5),
        'w_ffn_down': nrm(ks[23], (DEPTH, D_FF, D_MODEL), D_FF ** -0.5),
    }


def reference(x_prompt, x_sample, cache_a_k, cache_a_v, cache_b_k, cache_b_v, state_conv,
              norm_mix, norm_ffn, norm_final, w_attn_in, w_attn_out, rel_bias,
              lambda_q1, lambda_k1, lambda_q2, lambda_k2, subln_g,
              w_conv_in, conv_w, w_conv_out, w_ffn_gate, w_ffn_up, w_ffn_down):
    xp, xs = x_prompt, x_sample
    pak, pav, pbk, pbv, pcs = [], [], [], [], []
    sak, sav, sbk, sbv, scs = [], [], [], [], []
    for i in range(DEPTH):
        j = i // 2
        hp = rms_norm(xp, norm_mix[i])
        hs = rms_norm(xs, norm_mix[i])
        if i % 2 == 0:
            lam_init = 0.8 - 0.6 * math.exp(-0.3 * i)
            lam = diff_lambda(lambda_q1[j], lambda_k1[j], lambda_q2[j], lambda_k2[j], lam_init)
            mp, ka, va, kb, vb = attn_mixer_prompt(hp, w_attn_in[j], w_attn_out[j], rel_bias[j],
                                                   lam, subln_g[j], lam_init)
            pak.append(ka); pav.append(va); pbk.append(kb); pbv.append(vb)
            ms, ka, va, kb, vb = attn_mixer_sample(hs, cache_a_k[j], cache_a_v[j], cache_b_k[j], cache_b_v[j],
                                                   w_attn_in[j], w_attn_out[j], rel_bias[j],
                                                   lam, subln_g[j], lam_init)
            sak.append(ka); sav.append(va); sbk.append(kb); sbv.append(vb)
        else:
            mp, cp = conv_mixer(hp, w_conv_in[j], conv_w[j], w_conv_out[j], None)
            ms, cs = conv_mixer(hs, w_conv_in[j], conv_w[j], w_conv_out[j], state_conv[j])
            pcs.append(cp); scs.append(cs)
        xp = xp + mp
        xs = xs + ms
        xp = xp + swiglu(rms_norm(xp, norm_ffn[i]), w_ffn_gate[i], w_ffn_up[i], w_ffn_down[i])
        xs = xs + swiglu(rms_norm(xs, norm_ffn[i]), w_ffn_gate[i], w_ffn_up[i], w_ffn_down[i])
    y_prompt = rms_norm(xp, norm_final)
    y_sample = rms_norm(xs, norm_final)
    return (y_prompt, y_sample,
            jnp.stack(pak), jnp.stack(pav), jnp.stack(pbk), jnp.stack(pbv), jnp.stack(pcs),
            jnp.stack(sak), jnp.stack(sav), jnp.stack(sbk), jnp.stack(sbv), jnp.stack(scs))
```

```python
import math
from contextlib import ExitStack
import numpy as np
import concourse.bass as bass
import concourse.mybir as mybir
from concourse.bass_utils import run_bass_kernel_spmd

F32 = mybir.dt.float32
BF16 = mybir.dt.bfloat16
AF = mybir.ActivationFunctionType
ALU = mybir.AluOpType

D = 1024
DFF = 2816
NKC = 8
EPS = 1e-6
PE, ACT, DVE, POOL, SP = "pe", "act", "dve", "pool", "sp"
ENGS = (PE, ACT, DVE, POOL, SP)
EPOCH = 60000
NSLOT = 4
import os
DBG = os.environ.get("KDBG", "acfABS")
LVL = int(os.environ.get("KLVL", "9"))


class _Proxy:
    def __getattr__(self, name):
        def f(*a, **k):
            return (name, a, k)
        return f


_PROXY = _Proxy()


class Buf:
    __slots__ = ("w", "r", "x")

    def __init__(self, x=False):
        self.w = None
        self.r = {}
        self.x = x


class Sched:
    def __init__(self, nc, es):
        self.nc, self.es = nc, es
        self.ops = {e: [] for e in ENGS}
        self.semh = {}
        self.cnt = {e: 0 for e in ENGS}
        self.epoch = {e: 0 for e in ENGS}
        self.waited = {}
        self.dq = {}
        self.dcnt = {}
        self.nsem = 0
        for e in (PE, ACT, DVE, POOL):
            self.semh[e] = [self._newsem()]
        for q, n in ((SP, 16), (POOL, 8), (ACT, 2)):
            ids = []
            for i in range(n):
                pid = "d_%s_%d" % (q, i)
                self.semh[pid] = [self._newsem()]
                self.dcnt[pid] = 0
                ids.append(pid)
            self.dq[q] = [ids, 0]

    def _newsem(self):
        self.nsem += 1
        return self.es.enter_context(self.nc.semaphore("s%d" % self.nsem))

    def _need(self, cons, tok):
        if tok is None:
            return
        pid, ep, val = tok
        key = (cons, pid)
        if self.waited.get(key, (-1, -1)) >= (ep, val):
            return
        self.waited[key] = (ep, val)
        sem = self.semh[pid][ep]
        self.ops[cons].append(lambda e, sem=sem, val=val: e.wait_ge(sem, val))

    def _deps(self, cons, reads, writes, is_dma):
        for b in reads:
            if b.w is not None and not (cons == PE and b.w[0] == PE and not is_dma):
                self._need(cons, b.w)
            if b.x:
                for pid, tok in b.r.items():
                    if pid != cons:
                        self._need(cons, tok)
        for b in writes:
            if b.w is not None and not (cons == PE and b.w[0] == PE and not is_dma):
                self._need(cons, b.w)
            for pid, tok in b.r.items():
                if pid == cons and not is_dma:
                    continue
                self._need(cons, tok)

    def _mark(self, tok, reads, writes):
        for b in reads:
            b.r[tok[0]] = tok
        for b in writes:
            b.w = tok
            b.r = {}

    def op(self, eng, fn, reads=(), writes=()):
        self._deps(eng, reads, writes, False)
        if self.cnt[eng] >= EPOCH:
            self.epoch[eng] += 1
            self.cnt[eng] = 0
            self.semh[eng].append(self._newsem())
        self.cnt[eng] += 1
        tok = (eng, self.epoch[eng], self.cnt[eng])
        sem = self.semh[eng][self.epoch[eng]]
        r = fn(_PROXY)
        self.ops[eng].append(lambda e, r=r, sem=sem: getattr(e, r[0])(*r[1], **r[2]).then_inc(sem, 1))
        self._mark(tok, reads, writes)

    def dma(self, q, out, in_, reads=(), writes=(), **kw):
        self._deps(q, reads, writes, True)
        ids, rr = self.dq[q]
        pid = ids[rr % len(ids)]
        self.dq[q][1] = rr + 1
        if self.dcnt[pid]:
            self._need(q, (pid, 0, 16 * self.dcnt[pid]))
        self.dcnt[pid] += 1
        tok = (pid, 0, 16 * self.dcnt[pid])
        sem = self.semh[pid][0]
        self.ops[q].append(lambda e, out=out, in_=in_, sem=sem, kw=kw: e.dma_start(out=out, in_=in_, **kw).then_inc(sem, 16))
        self._mark(tok, reads, writes)

    def finish(self):
        for pid, n in self.dcnt.items():
            if n:
                sem = self.semh[pid][0]
                self.ops[SP].append(lambda e, sem=sem, n=n: e.wait_ge(sem, 16 * n))

    def emit(self):
        block = self.es.enter_context(self.nc.Block())
        ops = self.ops

        def run(lst):
            def f(e):
                for o in lst:
                    o(e)
            return f

        block.tensor(run(ops[PE]))
        block.scalar(run(ops[ACT]))
        block.vector(run(ops[DVE]))
        block.gpsimd(run(ops[POOL]))
        block.sync(run(ops[SP]))


def build(S, NSB=4, TQ=16, PAST=2048, APAST=512):
    TT = 512
    NT = S // TT
    NS = NSB * TQ
    nc = bass.Bass("TRN2", target_bir_lowering=False)
    es = ExitStack()
    with es:
        def din(name, shape, dt=F32):
            return nc.dram_tensor(name, list(shape), dt, kind="ExternalInput").ap()

        def dout(name, shape):
            return nc.dram_tensor(name, list(shape), F32, kind="ExternalOutput").ap()

        xp = din("xp", [S, D]); xs = din("xs", [NS, D])
        cak = din("cak", [2, NSB, APAST, 512]); cav = din("cav", [2, NSB, APAST, 512])
        cbk = din("cbk", [2, NSB, PAST, 512]); cbv = din("cbv", [2, NSB, PAST, 512])
        sconv = din("sconv", [2, NSB, 2, D])
        normT = din("normT", [128, 72]); gfin = din("gfin", [128, D])
        w_ain = din("w_ain", [2, D, 3072]); w_aout = din("w_aout", [2, D, D])
        w_cin = din("w_cin", [2, D, 3072]); w_cout = din("w_cout", [2, D, D])
        w_g = din("w_g", [4, D, DFF]); w_u = din("w_u", [4, D, DFF]); w_d = din("w_d", [4, DFF, D])
        convw = din("convw", [128, 48]); lam_bc = din("lam_bc", [128, 512]); subg_bc = din("subg_bc", [128, 256])
        biasP = din("biasP", [2, 128, 8 * 256]); biasC = din("biasC", [128, 16])
        biasS = din("biasS", [2, 128, 512]); biasSn = din("biasSn", [2, TQ, 128])
        rotP = din("rotP", [S, 128]); rotS = din("rotS", [NS, 128])
        ident_d = din("ident", [128, 128])
        yp = dout("yp", [S, D]); ys = dout("ys", [NS, D])
        pak = dout("pak", [2, 512, 512]); pav = dout("pav", [2, 512, 512])
        pbk = dout("pbk", [2, S, 512]); pbv = dout("pbv", [2, S, 512])
        pcs = dout("pcs", [2, 2, D])
        sak = dout("sak", [2, NS, 512]); sav = dout("sav", [2, NS, 512])
        sbk = dout("sbk", [2, NS, 512]); sbv = dout("sbv", [2, NS, 512])
        scs = dout("scs", [2, NSB, 2, D])
        kscr = nc.dram_tensor("kscr", [2, 4, NT, 128, 512], BF16, kind="ExternalOutput").ap()
        vscr = nc.dram_tensor("vscr", [2, 4, NT, 128, 516], BF16, kind="ExternalOutput").ap()
        kscr_b = [[[Buf() for _ in range(NT)] for _ in range(4)] for _ in range(2)]
        vscr_b = [[[Buf() for _ in range(NT)] for _ in range(4)] for _ in range(2)]

        sc = Sched(nc, es)

        def sb(name, shape, dt=F32):
            return es.enter_context(nc.sbuf_tensor(name, list(shape), dt))

        def ps(name, shape, dt=F32):
            return es.enter_context(nc.psum_tensor(name, list(shape), dt))

        X = sb("X", [128, 4, D]); Xb = [Buf() for _ in range(4)]
        hn = sb("hn", [128, 4, D], BF16); hnb = [Buf() for _ in range(4)]
        hT = sb("hT", [128, 8, TT], BF16); hTb = Buf()
        wsl = [sb("w%d" % i, [128, 8, 512], BF16) for i in range(NSLOT)]
        wslb = [Buf() for _ in range(NSLOT)]
        wrr = [0]
        KaT = [sb("KaT%d" % j, [128, 4, 2, TT], BF16) for j in range(2)]
        KaTb = [[Buf(), Buf()] for _ in range(2)]
        Vae = [sb("Vae%d" % j, [128, 2, 4, 8, 65], BF16) for j in range(2)]
        Vaeb = [[Buf(), Buf()] for _ in range(2)]
        EB = [sb("EB%d" % j, [128, 8, 256], BF16) for j in range(2)]; EBb = Buf()
        EBs = [sb("EBs%d" % j, [128, 512], BF16) for j in range(2)]
        EBn = [sb("EBn%d" % j, [TQ, 128], BF16) for j in range(2)]
        QaT = sb("QaT", [128, 4, TT], BF16); QaTb = Buf()
        QbT = sb("QbT", [128, 4, TT], BF16); QbTb = Buf()
        KbTc = sb("KbTc", [128, 4, TT], BF16); KbTcb = Buf()
        Vbc = sb("Vbc", [128, 4, 4, 129], BF16); Vbcb = Buf()
        KaTs = sb("KaTs", [128, 4, 64], BF16); KaTsb = Buf()
        qbr = sb("qbr", [128, 4, 512], BF16); qbrb = [Buf() for _ in range(4)]
        kbr, kbrb = qbr, qbrb
        NT32 = 6
        t32 = [sb("t32_%d" % i, [128, 512]) for i in range(NT32)]; t32b = [Buf() for _ in range(NT32)]
        t32rr = [0]
        NTB = 4
        tb16 = [sb("tb16_%d" % i, [128, 640], BF16) for i in range(NTB)]; tb16b = [Buf() for _ in range(NTB)]
        tbrr = [0]
        NKV = 3
        kst = [sb("kst%d" % i, [128, 512], BF16) for i in range(NKV)]; kstb = [Buf() for _ in range(NKV)]
        vst = [sb("vst%d" % i, [128, 4, 129], BF16) for i in range(NKV)]; vstb = [Buf() for _ in range(NKV)]
        kvrr = [0]
        big = sb("big", [128, 22 * 512], BF16); bigb = Buf()
        HT = big[:].rearrange("p (m t) -> p m t", m=22)
        xin = big[:, 0:2 * 8 * 514].bitcast(F32).rearrange("p (m t) -> p m t", m=8)
        xin_s = big[:, 0:2 * 8 * NSB * (TQ + 2)].bitcast(F32).rearrange("p (m b t) -> p m b t", m=8, b=NSB)
        carry = [sb("carry%d" % j, [128, 8, 2]) for j in range(2)]; carryb = [Buf(), Buf()]
        rot = sb("rot", [128, 4, 128]); rotb = Buf()
        normT_s = sb("normT_s", [128, 72]); gfin_s = sb("gfin_s", [128, D]); convw_s = sb("convw_s", [128, 48])
        lam_s = sb("lam_s", [128, 512]); subg_s = sb("subg_s", [128, 256]); biasC_s = sb("biasC_s", [128, 16])
        neglam = sb("neglam", [128, 2]); lamtmp = sb("lamtmp", [128, 8])
        ident_f = sb("ident_f", [128, 128]); ident = sb("ident_b", [128, 128], BF16)
        zt = sb("zt", [128, 384], BF16)
        ss = sb("ss", [128, 8]); ssb = Buf()
        rstd = sb("rstd", [128, 8]); rstdb = Buf()
        rl = sb("rl", [128, 16]); rlb = Buf()
        junk = qbr[:, 0:2, :].rearrange("p a c -> p (a c)")
        constb = Buf()
        SKraw = big[:, 0:2048]; SKrawb = bigb
        SKT = big[:, 2048:4096]; SKTb = bigb
        SVraw = big[:, 4096:6144]; SVrawb = bigb
        SVext = big[:, 6144:6144 + 2080]; SVextb = bigb
        SVnA = sb("SVnA", [TQ, 8, 65], BF16); SVnAb = Buf()
        SVnB = sb("SVnB", [TQ, 4, 129], BF16); SVnBb = Buf()
        cat_s = sb("cat_s", [TQ, D], BF16); cat_sb = Buf()

        P = [ps("P%d" % i, [128, 512]) for i in range(6)]; Pb = [Buf(True) for _ in range(6)]
        PT = [ps("PT%d" % i, [128, 1024], BF16) for i in range(2)]; PTb = [Buf(True) for _ in range(2)]
        ptrr = [0]

        def nt32():
            i = t32rr[0] % NT32; t32rr[0] += 1
            return t32[i], t32b[i]

        def ntb():
            i = tbrr[0] % NTB; tbrr[0] += 1
            return tb16[i], tb16b[i]

        def npt():
            i = ptrr[0] % 2; ptrr[0] += 1
            return PT[i], PTb[i]

        alt = [0]

        def evac(out, in_, reads, writes, scale=None):
            alt[0] += 1
            if alt[0] % 2:
                if scale is None:
                    sc.op(ACT, lambda e: e.activation(out=out, in_=in_, func=AF.Copy), reads, writes)
                else:
                    sc.op(ACT, lambda e: e.activation(out=out, in_=in_, func=AF.Copy, scale=scale), reads, writes)
            else:
                if scale is None:
                    sc.op(DVE, lambda e: e.tensor_copy(out=out, in_=in_), reads, writes)
                else:
                    sc.op(DVE, lambda e: e.tensor_scalar_mul(out=out, in0=in_, scalar1=scale), reads, writes)

        for dst, src in ((normT_s, normT), (gfin_s, gfin), (convw_s, convw), (lam_s, lam_bc), (subg_s, subg_bc),
                         (biasC_s, biasC), (ident_f, ident_d)):
            sc.dma(SP, dst[:], src[:, :], writes=[constb])
        sc.op(DVE, lambda e: e.tensor_copy(out=ident[:], in_=ident_f[:]), [constb], [constb])
        sc.op(DVE, lambda e: e.memset(zt[:], 0.0), [constb], [constb])
        LAM_INIT = [0.8 - 0.6 * math.exp(-0.3 * i) for i in (0, 2)]
        lv = lam_s[:].rearrange("p (j v d) -> p j v d", j=2, v=4)
        for j in range(2):
            for m in range(2):
                a, b = lv[:, j, 2 * m, :], lv[:, j, 2 * m + 1, :]
                t, tb_ = nt32()
                sc.op(DVE, lambda e, t=t, a=a, b=b: e.tensor_tensor(out=t[:, 0:64], in0=a, in1=b, op=ALU.mult), [constb], [tb_])
                col = lamtmp[:, j * 2 + m:j * 2 + m + 1]
                sc.op(DVE, lambda e, t=t, col=col: e.tensor_reduce(out=col, in_=t[:, 0:64], axis=mybir.AxisListType.X, op=ALU.add), [tb_], [constb])
            sc.op(ACT, lambda e, j=j: e.activation(out=lamtmp[:, 4 + j * 2:6 + j * 2], in_=lamtmp[:, j * 2:j * 2 + 2], func=AF.Exp), [constb], [constb])
            sc.op(DVE, lambda e, j=j: e.scalar_tensor_tensor(out=neglam[:, j:j + 1], in0=lamtmp[:, 5 + j * 2:6 + j * 2], scalar=-LAM_INIT[j],
                                                            in1=lamtmp[:, 4 + j * 2:5 + j * 2], op0=ALU.add, op1=ALU.subtract), [constb], [constb])
            sc.op(DVE, lambda e, j=j: e.tensor_scalar_mul(out=subg_s[:, j * 128:(j + 1) * 128], in0=subg_s[:, j * 128:(j + 1) * 128], scalar1=1.0 - LAM_INIT[j]), [constb], [constb])
        for j in range(2):
            for hh in range(4):
                t, tb_ = nt32()
                sc.dma(SP, t[:, :], biasP[j, :, hh * 512:(hh + 1) * 512], writes=[tb_])
                sc.op(ACT, lambda e, t=t, j=j, hh=hh: e.activation(out=EB[j][:, 2 * hh:2 * hh + 2, :], in_=t[:, :].rearrange("p (a b) -> p a b", a=2), func=AF.Exp), [tb_], [EBb])
            sc.op(DVE, lambda e, j=j: e.memset(EB[j][64:128, :, 128:192], 0.0), [EBb], [EBb])
            t, tb_ = nt32()
            sc.dma(SP, t[:, :], biasS[j, :, :], writes=[tb_])
            sc.op(ACT, lambda e, t=t, j=j: e.activation(out=EBs[j][:, :], in_=t[:, :], func=AF.Exp), [tb_], [EBb])
            t, tb_ = nt32()
            sc.dma(SP, t[0:TQ, 0:128], biasSn[j, :, :], writes=[tb_])
            sc.op(ACT, lambda e, t=t, j=j: e.activation(out=EBn[j][:, :], in_=t[0:TQ, 0:128], func=AF.Exp), [tb_], [EBb])
        for j in range(2):
            sc.op(DVE, lambda e, j=j: e.memset(Vae[j][:, :, :, :, 64:65], 1.0), [], [Vaeb[j][0], Vaeb[j][1]])
            sc.op(DVE, lambda e, j=j: e.memset(carry[j][:], 0.0), [], [carryb[j]])
        sc.op(DVE, lambda e: e.memset(Vbc[:, :, :, 128:129], 1.0), [], [Vbcb])
        sc.op(DVE, lambda e: e.memset(SVnA[:, :, 64:65], 1.0), [], [SVnAb])
        sc.op(DVE, lambda e: e.memset(SVnB[:, :, 128:129], 1.0), [], [SVnBb])

        def wload(Wap, k0, nk, c0, ncol):
            i = wrr[0] % NSLOT; wrr[0] += 1
            src = Wap[k0 * 128:(k0 + nk) * 128, c0:c0 + ncol].rearrange("(kc p) n -> p kc n", p=128)
            sc.dma(POOL, wsl[i][:, 0:nk, 0:ncol], src, writes=[wslb[i]])
            return wsl[i], wslb[i]

        def norm_to_hT(tile, vec):
            nb, bp, ntok = tile["nb"], tile["bp"], tile["ntok"]
            sc.op(DVE, lambda e: e.memset(ss[:, 0:nb], 0.0), [], [ssb])
            for b in range(nb):
                sc.op(ACT, lambda e, b=b: e.activation(out=junk[:bp, :], in_=X[:bp, b, :], func=AF.Square, accum_out=ss[:bp, b:b + 1]), [Xb[b], ssb], [ssb])
            sc.op(ACT, lambda e: e.activation(out=rstd[:bp, 0:nb], in_=ss[:bp, 0:nb], func=AF.Ln, scale=1.0 / D, bias=EPS), [ssb], [rstdb])
            sc.op(ACT, lambda e: e.activation(out=rstd[:bp, 0:nb], in_=rstd[:bp, 0:nb], func=AF.Exp, scale=-0.5), [rstdb], [rstdb])
            for b in range(nb):
                if b % 2:
                    sc.op(ACT, lambda e, b=b: e.activation(out=hn[:bp, b, :], in_=X[:bp, b, :], func=AF.Copy, scale=rstd[:bp, b:b + 1]), [Xb[b], rstdb], [hnb[b]])
                else:
                    sc.op(DVE, lambda e, b=b: e.tensor_scalar_mul(out=hn[:bp, b, :], in0=X[:bp, b, :], scalar1=rstd[:bp, b:b + 1]), [Xb[b], rstdb], [hnb[b]])
            tm_to_fm(tile, hn, hnb, gvec=vec)

        def tm_to_fm(tile, src, srcb, gvec=None):
            nb, bp, ntok = tile["nb"], tile["bp"], tile["ntok"]
            for kc in range(NKC):
                pt, ptb = npt()
                for b in range(nb):
                    sc.op(PE, lambda e, pt=pt, b=b, kc=kc: e.transpose(pt[:, b * bp:(b + 1) * bp], src[:bp, b, kc * 128:(kc + 1) * 128], ident[:bp, :bp]), [srcb[b], constb], [ptb])
                scale = None if gvec is None else normT_s[:, gvec * 8 + kc:gvec * 8 + kc + 1]
                evac(hT[:, kc, 0:ntok], pt[:, 0:ntok], [ptb, constb], [hTb], scale)

        def proj_tm(tile, Wap, K, c0, ncols, cb_fn, lhs=None, lhsb=None, nb=None, bp=None, boff=None):
            lhs = hT if lhs is None else lhs
            lhsbs = [hTb] if lhsb is None else list(lhsb)
            nb = tile["nb"] if nb is None else nb
            bp = tile["bp"] if bp is None else bp
            nkc = K // 128
            groups = [(k, min(8, nkc - k)) for k in range(0, nkc, 8)]
            for cb in range((ncols + 511) // 512):
                cc = c0 + cb * 512
                ncol = min(512, c0 + ncols - cc)
                for (k0, nk) in groups:
                    w, wb_ = wload(Wap, k0, nk, cc, ncol)
                    for b in range(nb):
                        pi = b % 4
                        col0 = (b * bp) if boff is None else boff(b)
                        for kk in range(nk):
                            kc = k0 + kk
                            sc.op(PE, lambda e, pi=pi, w=w, kk=kk, kc=kc, col0=col0, ncol=ncol: e.matmul(
                                P[pi][:bp, 0:ncol], lhs[:, kc, col0:col0 + bp], w[:, kk, 0:ncol], start=(kc == 0), stop=(kc == nkc - 1)),
                                lhsbs + [wb_], [Pb[pi]])
                for b in range(nb):
                    cb_fn(cb, b, P[b % 4], Pb[b % 4], ncol)

        def proj_fm(tile, Wap, c0, ncols, cb_fn, banks=(0, 1)):
            ntok = tile["ntok"]
            mi = 0
            for cb in range((ncols + 511) // 512):
                cc = c0 + cb * 512
                ncol = min(512, c0 + ncols - cc)
                w, wb_ = wload(Wap, 0, 8, cc, ncol)
                for m in range(ncol // 128):
                    pi = banks[mi % len(banks)]
                    for kc in range(8):
                        sc.op(PE, lambda e, pi=pi, w=w, kc=kc, m=m: e.matmul(P[pi][:, 0:ntok], w[:, kc, m * 128:(m + 1) * 128], hT[:, kc, 0:ntok],
                                                                         start=(kc == 0), stop=(kc == 7)), [hTb, wb_], [Pb[pi]])
                    cb_fn(mi, P[pi], Pb[pi])
                    mi += 1

        def add_resid(tile):
            bp = tile["bp"]

            def f(cb, b, p, pb, ncol):
                sc.op(DVE, lambda e: e.tensor_tensor(out=X[:bp, b, cb * 512:cb * 512 + ncol], in0=p[:bp, 0:ncol], in1=X[:bp, b, cb * 512:cb * 512 + ncol], op=ALU.add), [pb, Xb[b]], [Xb[b]])
            return f

        def rotary(tile, p, pb, b, out_bf, out_bfb, out_f32):
            bp = tile["bp"]
            raw, rawb = nt32()
            sc.op(ACT, lambda e: e.activation(out=raw[:bp, :], in_=p[:bp, :], func=AF.Copy), [pb], [rawb])
            rv = raw[:bp, :].rearrange("p (g d) -> p g d", g=8)
            c2 = rot[:bp, b, 0:64].unsqueeze(1).to_broadcast([bp, 8, 64])
            s2a = rot[:bp, b, 64:96].unsqueeze(1).to_broadcast([bp, 8, 32])
            s2b = rot[:bp, b, 96:128].unsqueeze(1).to_broadcast([bp, 8, 32])
            A, Ab = nt32()
            Bt, Btb = nt32()
            Av = A[:bp, :].rearrange("p (g d) -> p g d", g=8)
            Bv = Bt[:bp, :].rearrange("p (g d) -> p g d", g=8)
            sc.op(DVE, lambda e: e.tensor_tensor(out=Av, in0=rv, in1=c2, op=ALU.mult), [rawb, rotb], [Ab])
            sc.op(DVE, lambda e: e.tensor_tensor(out=Bv[:, :, 0:32], in0=rv[:, :, 32:64], in1=s2a, op=ALU.mult), [rawb, rotb], [Btb])
            sc.op(DVE, lambda e: e.tensor_tensor(out=Bv[:, :, 32:64], in0=rv[:, :, 0:32], in1=s2b, op=ALU.mult), [rawb, rotb, Btb], [Btb])
            if out_f32:
                sc.op(DVE, lambda e: e.tensor_tensor(out=A[:bp, :], in0=A[:bp, :], in1=Bt[:bp, :], op=ALU.add), [Ab, Btb], [Ab])
                sc.op(ACT, lambda e: e.activation(out=out_bf, in_=A[:bp, :], func=AF.Copy), [Ab], [out_bfb])
                return A, Ab
            sc.op(DVE, lambda e: e.tensor_tensor(out=out_bf, in0=A[:bp, :], in1=Bt[:bp, :], op=ALU.add), [Ab, Btb], [out_bfb])
            return None, None

        def ffn(tile, i):
            ntok = tile["ntok"]
            norm_to_hT(tile, 4 + i)
            for cb in range(6):
                cc = cb * 512
                ncol = min(512, DFF - cc)
                wg_, wgb = wload(w_g[i], 0, 8, cc, ncol)
                wu_, wub = wload(w_u[i], 0, 8, cc, ncol)
                for m in range(ncol // 128):
                    gm = cb * 4 + m
                    pg, pu = (0, 1) if gm % 2 == 0 else (2, 3)
                    for kc in range(8):
                        sc.op(PE, lambda e, pg=pg, kc=kc, m=m, w=wg_: e.matmul(P[pg][:, 0:ntok], w[:, kc, m * 128:(m + 1) * 128], hT[:, kc, 0:ntok], start=(kc == 0), stop=(kc == 7)), [hTb, wgb], [Pb[pg]])
                    for kc in range(8):
                        sc.op(PE, lambda e, pu=pu, kc=kc, m=m, w=wu_: e.matmul(P[pu][:, 0:ntok], w[:, kc, m * 128:(m + 1) * 128], hT[:, kc, 0:ntok], start=(kc == 0), stop=(kc == 7)), [hTb, wub], [Pb[pu]])
                    sg, sgb = nt32()
                    sc.op(ACT, lambda e, sg=sg, pg=pg: e.activation(out=sg[:, 0:ntok], in_=P[pg][:, 0:ntok], func=AF.Silu), [Pb[pg]], [sgb])
                    sc.op(DVE, lambda e, sg=sg, pu=pu, gm=gm: e.tensor_tensor(out=HT[:, gm, 0:ntok], in0=P[pu][:, 0:ntok], in1=sg[:, 0:ntok], op=ALU.mult), [Pb[pu], sgb], [bigb])
            proj_tm(tile, w_d[i], DFF, 0, D, add_resid(tile), lhs=HT, lhsb=[bigb])

        def conv(tile, i):
            j = i // 2
            ntok, isS = tile["ntok"], tile["kind"] == "s"
            norm_to_hT(tile, i)
            if isS:
                for bb in range(NSB):
                    for r in range(2):
                        sc.dma(SP, xin_s[:, :, bb, r], sconv[j, bb, r, :].rearrange("(m p) -> p m", p=128), writes=[bigb], allow_slow_non_contiguous=True)

                def xv(m, lo):
                    return xin_s[:, m, :, lo:lo + TQ]

                def pv(ap):
                    return ap.rearrange("p (b t) -> p b t", b=NSB)
            else:
                sc.op(DVE, lambda e: e.tensor_copy(out=xin[:, :, 0:2], in_=carry[j][:]), [carryb[j]], [bigb])

                def xv(m, lo):
                    return xin[:, m, lo:lo + ntok]

                def pv(ap):
                    return ap
            for cb in range(2):
                wb3 = [wload(w_cin[j], 0, 8, part * 1024 + cb * 512, 512) for part in range(3)]
                for m4 in range(4):
                    m = cb * 4 + m4
                    bk = (0, 1, 2) if m % 2 == 0 else (3, 4, 5)
                    for part in range(3):
                        w, wb_ = wb3[part]
                        pi = bk[part]
                        for kc in range(8):
                            sc.op(PE, lambda e, pi=pi, w=w, kc=kc, m4=m4: e.matmul(P[pi][:, 0:ntok], w[:, kc, m4 * 128:(m4 + 1) * 128], hT[:, kc, 0:ntok], start=(kc == 0), stop=(kc == 7)), [hTb, wb_], [Pb[pi]])
                    gbs, gbsb = nt32()
                    gcs, gcsb = nt32()
                    tmp, tmpb = nt32()
                    sc.op(ACT, lambda e, t=gbs, pi=bk[0]: e.activation(out=t[:, 0:ntok], in_=P[pi][:, 0:ntok], func=AF.Copy), [Pb[bk[0]]], [gbsb])
                    sc.op(ACT, lambda e, t=gcs, pi=bk[1]: e.activation(out=t[:, 0:ntok], in_=P[pi][:, 0:ntok], func=AF.Copy), [Pb[bk[1]]], [gcsb])
                    sc.op(DVE, lambda e, m=m, t=gcs, pi=bk[2]: e.tensor_tensor(out=xv(m, 2), in0=pv(P[pi][:, 0:ntok]), in1=pv(t[:, 0:ntok]), op=ALU.mult), [Pb[bk[2]], gcsb], [bigb])
                    wc = [convw_s[:, (j * 3 + k) * 8 + m:(j * 3 + k) * 8 + m + 1] for k in range(3)]
                    sc.op(DVE, lambda e, m=m, t=tmp, wc=wc: e.tensor_scalar_mul(out=pv(t[:, 0:ntok]), in0=xv(m, 2), scalar1=wc[2]), [bigb], [tmpb])
                    sc.op(DVE, lambda e, m=m, t=tmp, wc=wc: e.scalar_tensor_tensor(out=pv(t[:, 0:ntok]), in0=xv(m, 1), scalar=wc[1], in1=pv(t[:, 0:ntok]), op0=ALU.mult, op1=ALU.add), [bigb, tmpb], [tmpb])
                    sc.op(DVE, lambda e, m=m, t=tmp, wc=wc: e.scalar_tensor_tensor(out=pv(t[:, 0:ntok]), in0=xv(m, 0), scalar=wc[0], in1=pv(t[:, 0:ntok]), op0=ALU.mult, op1=ALU.add), [bigb, tmpb], [tmpb])
                    sc.op(DVE, lambda e, m=m, t=tmp, g=gbs: e.tensor_tensor(out=hn[:, m // 2, (m % 2) * 512:(m % 2) * 512 + ntok], in0=t[:, 0:ntok], in1=g[:, 0:ntok], op=ALU.mult), [tmpb, gbsb], [hnb[m // 2]])
            if isS:
                for bb in range(NSB):
                    for r in range(2):
                        sc.dma(SP, scs[j, bb, r, :].rearrange("(m p) -> p m", p=128), xin_s[:, :, bb, TQ + r], reads=[bigb], allow_slow_non_contiguous=True)
            else:
                sc.op(DVE, lambda e: e.tensor_copy(out=carry[j][:], in_=xin[:, :, ntok:ntok + 2]), [bigb], [carryb[j]])
                if tile["last"]:
                    for r in range(2):
                        sc.dma(SP, pcs[j, r, :].rearrange("(m p) -> p m", p=128), xin[:, :, ntok + r], reads=[bigb], allow_slow_non_contiguous=True)
            YT = hn[:].rearrange("p a (c t) -> p (a c) t", c=2)

            proj_tm(tile, w_cout[j], D, 0, D, add_resid(tile), lhs=YT, lhsb=hnb)

        def attn(tile, i):
            j = i // 2
            t = tile["idx"]
            nb, bp, ntok, isS = tile["nb"], tile["bp"], tile["ntok"], tile["kind"] == "s"
            par = t % 2
            r0 = t * TT
            norm_to_hT(tile, i)
            Win = w_ain[j]
            if isS:
                sc.dma(SP, rot[0:NS, 0, :], rotS[:, :], writes=[rotb])
            else:
                sc.dma(SP, rot[:, :, :], rotP[r0:r0 + TT, :].rearrange("(b p) c -> p b c", p=128), writes=[rotb])
            proj_fm(tile, Win, 0, 512, lambda mi, p, pb: evac(QaT[:, mi, 0:ntok], p[:, 0:ntok], [pb], [QaTb]))
            if isS:
                proj_fm(tile, Win, 512, 512, lambda mi, p, pb: evac(KaTs[:, mi, 0:ntok], p[:, 0:ntok], [pb], [KaTsb]), banks=(2, 3))
            else:
                proj_fm(tile, Win, 512, 512, lambda mi, p, pb: evac(KaT[j][:, mi, par, 0:ntok], p[:, 0:ntok], [pb], [KaTb[j][par]]), banks=(2, 3))
            if LVL < 2:
                return
            if isS or tile["last"]:
                def ka_out(cb, b, p, pb, ncol):
                    o, ob = nt32()
                    evac(o[:bp, :], p[:bp, :], [pb], [ob])
                    dst = sak[j, 0:NS, :] if isS else pak[j, b * 128:(b + 1) * 128, :]
                    sc.dma(SP, dst, o[:bp, :], reads=[ob])
                proj_tm(tile, Win, D, 512, 512, ka_out)

            def va_cb(cb, b, p, pb, ncol):
                if not isS:
                    sc.op(DVE, lambda e: e.tensor_copy(out=Vae[j][:, par, b, :, 0:64], in_=p[:, :].rearrange("p (h d) -> p h d", h=8)), [pb], [Vaeb[j][par]])
                if isS or tile["last"]:
                    o, ob = nt32()
                    sc.op(ACT, lambda e: e.activation(out=o[:bp, :], in_=p[:bp, :], func=AF.Copy), [pb], [ob])
                    dst = sav[j, 0:NS, :] if isS else pav[j, b * 128:(b + 1) * 128, :]
                    sc.dma(SP, dst, o[:bp, :], reads=[ob])
            proj_tm(tile, Win, D, 1024, 512, va_cb)

            if LVL < 3:
                return
            def qb_cb(cb, b, p, pb, ncol):
                rotary(tile, p, pb, b, qbr[:bp, b, :], qbrb[b], False)
            proj_tm(tile, Win, D, 1536, 512, qb_cb)

            def tr_qk(src, srcb, dst, dstb):
                for h in range(4):
                    pt, ptb = npt()
                    for b in range(nb):
                        sc.op(PE, lambda e, pt=pt, b=b, h=h, src=src: e.transpose(pt[:, b * bp:(b + 1) * bp], src[:bp, b, h * 128:(h + 1) * 128], ident[:bp, :bp]), [srcb[b], constb], [ptb])
                    evac(dst[:, h, 0:ntok], pt[:, 0:ntok], [ptb], [dstb])
            tr_qk(qbr, qbrb, QbT, QbTb)

            if LVL < 4:
                return

            def kb_cb(cb, b, p, pb, ncol):
                A, Ab = rotary(tile, p, pb, b, kbr[:bp, b, :], kbrb[b], True)
                dst = sbk[j, 0:NS, :] if isS else pbk[j, r0 + b * 128:r0 + (b + 1) * 128, :]
                sc.dma(SP, dst, A[:bp, :], reads=[Ab])
            proj_tm(tile, Win, D, 2048, 512, kb_cb)
            tr_qk(kbr, kbrb, KbTc, KbTcb)
            if not isS and LVL >= 5:
                for h in range(4):
                    sc.dma(SP, kscr[j, h, t, :, :], KbTc[:, h, :], reads=[KbTcb], writes=[kscr_b[j][h][t]])
            if LVL < 6:
                return

            def vb_cb(cb, b, p, pb, ncol):
                if not isS:
                    sc.op(DVE, lambda e: e.tensor_copy(out=Vbc[:, b, :, 0:128], in_=p[:, :].rearrange("p (h d) -> p h d", h=4)), [pb], [Vbcb])
                o, ob = nt32()
                sc.op(ACT, lambda e: e.activation(out=o[:bp, :], in_=p[:bp, :], func=AF.Copy), [pb], [ob])
                dst = sbv[j, 0:NS, :] if isS else pbv[j, r0 + b * 128:r0 + (b + 1) * 128, :]
                sc.dma(SP, dst, o[:bp, :], reads=[ob])
            proj_tm(tile, Win, D, 2560, 512, vb_cb)
            if not isS and LVL >= 7:
                for h in range(4):
                    sc.dma(SP, vscr[j, h, t, :, :].rearrange("p (b c) -> p b c", b=4), Vbc[:, :, h, :], reads=[Vbcb], writes=[vscr_b[j][h][t]])
            if LVL < 8:
                return

            if isS:
                if "S" in DBG:
                    attn_sample(tile, i)
            else:
                if "A" in DBG:
                    attn_A(tile, i)
                if "B" in DBG:
                    attn_B(tile, i)
            if not isS:
                tm_to_fm(tile, hn, hnb)
            proj_tm(tile, w_aout[j], D, 0, D, add_resid(tile))

        def attn_A(tile, i):
            j = i // 2
            t = tile["idx"]
            par = t % 2
            for qb in range(4):
                po = (4, 5)
                for h in range(8):
                    c, base = h // 2, (h % 2) * 64
                    pa, pb2 = (0, 1) if h % 2 == 0 else (2, 3)
                    jjs = [jj for jj in range(5) if not (t == 0 and qb + jj < 4)]
                    for jj in jjs:
                        w = qb + jj
                        wpar, wblk = (par ^ 1, w) if w < 4 else (par, w - 4)
                        pi, off = (pa, jj * 128) if jj < 4 else (pb2, 0)
                        sc.op(PE, lambda e, pi=pi, off=off, c=c, base=base, wpar=wpar, wblk=wblk: e.matmul(
                            P[pi][:, off:off + 128], KaT[j][base:base + 64, c, wpar, wblk * 128:(wblk + 1) * 128], QaT[base:base + 64, c, qb * 128:(qb + 1) * 128], start=True, stop=True),
                            [KaTb[j][wpar], QaTb], [Pb[pi]])
                    pt_, ptb_ = ntb()
                    lo = jjs[0]
                    if lo < 3:
                        sc.op(ACT, lambda e, pt_=pt_, lo=lo, pa=pa, h=h: e.activation(out=pt_[:, lo * 128:384], in_=P[pa][:, lo * 128:384], func=AF.Exp, scale=0.125, bias=biasC_s[:, j * 8 + h:j * 8 + h + 1]), [Pb[pa], constb], [ptb_])
                    sc.op(ACT, lambda e, pt_=pt_, pa=pa: e.activation(out=pt_[:, 384:512], in_=P[pa][:, 384:512], func=AF.Exp, scale=0.125), [Pb[pa]], [ptb_])
                    sc.op(ACT, lambda e, pt_=pt_, pb2=pb2: e.activation(out=pt_[:, 512:640], in_=P[pb2][:, 0:128], func=AF.Exp, scale=0.125), [Pb[pb2]], [ptb_])
                    sc.op(DVE, lambda e, pt_=pt_, h=h: e.tensor_tensor(out=pt_[:, 384:640], in0=pt_[:, 384:640], in1=EB[j][:, h, :], op=ALU.mult), [ptb_, EBb], [ptb_])
                    if lo == 0:
                        sc.op(DVE, lambda e, pt_=pt_: e.memset(pt_[0:64, 64:128], 0.0), [ptb_], [ptb_])
                    for jj in jjs:
                        w = qb + jj
                        wpar, wblk = (par ^ 1, w) if w < 4 else (par, w - 4)
                        pk = po[h // 4]
                        sc.op(PE, lambda e, pt_=pt_, jj=jj, pk=pk, h=h, wpar=wpar, wblk=wblk: e.matmul(
                            P[pk][:, (h % 4) * 65:(h % 4) * 65 + 65], pt_[:, jj * 128:(jj + 1) * 128], Vae[j][:, wpar, wblk, h, :], start=(jj == jjs[0]), stop=(jj == jjs[-1])),
                            [ptb_, Vaeb[j][wpar]], [Pb[pk]])
                for half in range(2):
                    pk = po[half]
                    ov = P[pk][:, 0:260].rearrange("p (h c) -> p h c", h=4)
                    sc.op(DVE, lambda e, ov=ov, half=half: e.reciprocal(out=rl[:, half * 4:half * 4 + 4], in_=ov[:, :, 64]), [Pb[pk]], [rlb])
                    sc.op(DVE, lambda e, ov=ov, half=half: e.tensor_tensor(out=hn[:, qb, half * 256:(half + 1) * 256].rearrange("p (h d) -> p h d", h=4), in0=ov[:, :, 0:64],
                                                                         in1=rl[:, half * 4:half * 4 + 4].unsqueeze(2).to_broadcast([128, 4, 64]), op=ALU.mult), [Pb[pk], rlb], [hnb[qb]])

        def diff_combine(np_, O0, O0b, O1, O1b, nq, j, dst_fn, dstb_fn):
            sc.op(DVE, lambda e: e.reciprocal(out=rl[:np_, 0:nq], in_=O0[:, :, 128]), [O0b], [rlb])
            sc.op(DVE, lambda e: e.reciprocal(out=rl[:np_, 4:4 + nq], in_=O1[:, :, 128]), [O1b], [rlb])
            sc.op(DVE, lambda e: e.tensor_scalar_mul(out=rl[:np_, 4:4 + nq], in0=rl[:np_, 4:4 + nq], scalar1=neglam[:np_, j:j + 1]), [rlb, constb], [rlb])
            sc.op(DVE, lambda e: e.memset(ss[:np_, 4:4 + nq], 0.0), [], [ssb])
            os_ = []
            for q in range(nq):
                o1, o1b = nt32()
                sc.op(DVE, lambda e, q=q, o1=o1: e.tensor_scalar_mul(out=o1[:np_, 0:128], in0=O0[:, q, 0:128], scalar1=rl[:np_, q:q + 1]), [O0b, rlb], [o1b])
                sc.op(DVE, lambda e, q=q, o1=o1: e.scalar_tensor_tensor(out=o1[:np_, 0:128], in0=O1[:, q, 0:128], scalar=rl[:np_, 4 + q:5 + q], in1=o1[:np_, 0:128], op0=ALU.mult, op1=ALU.add), [O1b, rlb, o1b], [o1b])
                sc.op(ACT, lambda e, q=q, o1=o1: e.activation(out=junk[:np_, 0:128], in_=o1[:np_, 0:128], func=AF.Square, accum_out=ss[:np_, 4 + q:5 + q]), [o1b, ssb], [ssb])
                os_.append((o1, o1b))
            sc.op(ACT, lambda e: e.activation(out=rstd[:np_, 4:4 + nq], in_=ss[:np_, 4:4 + nq], func=AF.Ln, scale=1.0 / 128, bias=EPS), [ssb], [rstdb])
            sc.op(ACT, lambda e: e.activation(out=rstd[:np_, 4:4 + nq], in_=rstd[:np_, 4:4 + nq], func=AF.Exp, scale=-0.5), [rstdb], [rstdb])
            for q in range(nq):
                o1, o1b = os_[q]
                sc.op(DVE, lambda e, q=q, o1=o1: e.scalar_tensor_tensor(out=dst_fn(q), in0=o1[:np_, 0:128], scalar=rstd[:np_, 4 + q:5 + q], in1=subg_s[:np_, j * 128:(j + 1) * 128], op0=ALU.mult, op1=ALU.mult),
                      [o1b, rstdb, constb], [dstb_fn(q)])

        def attn_B(tile, i):
            j = i // 2
            t = tile["idx"]
            for h in range(4):
                Ov = [[P[2][:, 0:258].rearrange("p (q c) -> p q c", q=2), P[3][:, 0:258].rearrange("p (q c) -> p q c", q=2)],
                      [P[4][:, 0:258].rearrange("p (q c) -> p q c", q=2), P[5][:, 0:258].rearrange("p (q c) -> p q c", q=2)]]
                Obk = [[2, 3], [4, 5]]
                srr = 0
                for bk in (2, 3, 4, 5):
                    sc.op(PE, lambda e, bk=bk: e.matmul(P[bk][:, 0:258], zt[:, 0:128], zt[:, 0:258], start=True, stop=False), [constb], [Pb[bk]])
                for kblk in range(t + 1):
                    si = kvrr[0] % NKV; kvrr[0] += 1
                    sc.dma(SP, kst[si][:, :], kscr[j, h, kblk, :, :], reads=[kscr_b[j][h][kblk]], writes=[kstb[si]])
                    sc.dma(SP, vst[si][:, :, :], vscr[j, h, kblk, :, :].rearrange("p (b c) -> p b c", b=4), reads=[vscr_b[j][h][kblk]], writes=[vstb[si]])
                    for kt in range(4):
                        diag = (kblk == t)
                        qlo = kt * 128 if diag else 0
                        for m in range(2):
                            pi = srr % 2; srr += 1
                            sc.op(PE, lambda e, pi=pi, si=si, m=m, kt=kt, qlo=qlo: e.matmul(P[pi][:, qlo:512], kst[si][m * 64:(m + 1) * 64, kt * 128:(kt + 1) * 128], QbT[m * 64:(m + 1) * 64, h, qlo:512], start=True, stop=True),
                                  [kstb[si], QbTb], [Pb[pi]])
                            pt_, ptb_ = ntb()
                            sc.op(ACT, lambda e, pt_=pt_, pi=pi, qlo=qlo: e.activation(out=pt_[:, qlo:512], in_=P[pi][:, qlo:512], func=AF.Exp, scale=0.125), [Pb[pi]], [ptb_])
                            if diag:
                                sc.op(DVE, lambda e, pt_=pt_, qlo=qlo: e.memset(pt_[64:128, qlo:qlo + 64], 0.0), [ptb_], [ptb_])
                            for qb in range(qlo // 128, 4):
                                bk = Obk[m][qb // 2]
                                first = (kblk == 0 and kt == 0)
                                last = (diag and kt == qb)
                                sc.op(PE, lambda e, pt_=pt_, qb=qb, m=m, si=si, kt=kt, first=first, last=last: e.matmul(Ov[m][qb // 2][:, qb % 2, :], pt_[:, qb * 128:(qb + 1) * 128], vst[si][:, kt, :], start=False, stop=(last and qb % 2 == 1)),
                                      [ptb_, vstb[si]], [Pb[bk]])
                for half in range(2):
                    diff_combine(128, Ov[0][half], Pb[Obk[0][half]], Ov[1][half], Pb[Obk[1][half]], 2, j,
                                 lambda q, half=half: hn[:, half * 2 + q, 512 + h * 128:512 + (h + 1) * 128], lambda q, half=half: hnb[half * 2 + q])

        def attn_sample(tile, i):
            j = i // 2
            tq = TQ
            for bb in range(NSB):
                cs0 = bb * tq
                def vnA(cb, b, p, pb, ncol):
                    sc.op(DVE, lambda e: e.tensor_copy(out=SVnA[:, :, 0:64], in_=p[:tq, :].rearrange("p (h d) -> p h d", h=8)), [pb], [SVnAb])
                proj_tm(tile, w_ain[j], D, 1024, 512, vnA, nb=1, bp=tq, boff=lambda b: cs0)

                def vnB(cb, b, p, pb, ncol):
                    sc.op(DVE, lambda e: e.tensor_copy(out=SVnB[:, :, 0:128], in_=p[:tq, :].rearrange("p (h d) -> p h d", h=4)), [pb], [SVnBb])
                proj_tm(tile, w_ain[j], D, 2560, 512, vnB, nb=1, bp=tq, boff=lambda b: cs0)
                sc.dma(POOL, SKraw[:, :].rearrange("p (k f) -> p k f", k=4), cak[j, bb, :, :].rearrange("(k p) f -> p k f", p=128), writes=[SKrawb])
                sc.dma(POOL, SVraw[:, :].rearrange("p (k f) -> p k f", k=4), cav[j, bb, :, :].rearrange("(k p) f -> p k f", p=128), writes=[SVrawb])
                skr = SKraw[:, :].rearrange("p (k f) -> p k f", k=4)
                for c in range(4):
                    pt, ptb = npt()
                    for kt in range(4):
                        sc.op(PE, lambda e, pt=pt, kt=kt, c=c: e.transpose(pt[:, kt * 128:(kt + 1) * 128], skr[:, kt, c * 128:(c + 1) * 128], ident[:, :]), [SKrawb, constb], [ptb])
                    evac(SKT[:, c * 512:(c + 1) * 512], pt[:, 0:512], [ptb], [SKTb])
                sve = SVext[:, 0:4 * 8 * 65].rearrange("p (k h c) -> p k h c", k=4, h=8)
                sc.op(DVE, lambda e: e.tensor_copy(out=sve[:, :, :, 0:64], in_=SVraw[:, :].rearrange("p (k h d) -> p k h d", k=4, h=8)), [SVrawb], [SVextb])
                sc.op(DVE, lambda e: e.memset(sve[:, :, :, 64:65], 1.0), [SVextb], [SVextb])
                for h in range(8):
                    c, base = h // 2, (h % 2) * 64
                    for kt in range(4):
                        sc.op(PE, lambda e, h=h, kt=kt, c=c, base=base: e.matmul(P[0][:, (h * 4 + kt) * tq:(h * 4 + kt + 1) * tq], SKT[base:base + 64, c * 512 + kt * 128:c * 512 + (kt + 1) * 128], QaT[base:base + 64, c, cs0:cs0 + tq], start=True, stop=True),
                              [SKTb, QaTb], [Pb[0]])
                    sc.op(PE, lambda e, h=h, c=c, base=base: e.matmul(P[1][0:tq, h * tq:(h + 1) * tq], KaTs[base:base + 64, c, cs0:cs0 + tq], QaT[base:base + 64, c, cs0:cs0 + tq], start=True, stop=True),
                          [KaTsb, QaTb], [Pb[1]])
                pt_, ptb_ = ntb()
                pn_, pnb_ = ntb()
                sc.op(ACT, lambda e, pt_=pt_: e.activation(out=pt_[:, 0:512], in_=P[0][:, 0:512], func=AF.Exp, scale=0.125), [Pb[0]], [ptb_])
                sc.op(DVE, lambda e, pt_=pt_: e.tensor_tensor(out=pt_[:, 0:512], in0=pt_[:, 0:512], in1=EBs[j][:, :], op=ALU.mult), [ptb_, EBb], [ptb_])
                sc.op(ACT, lambda e, pn_=pn_: e.activation(out=pn_[0:tq, 0:128], in_=P[1][0:tq, 0:128], func=AF.Exp, scale=0.125), [Pb[1]], [pnb_])
                sc.op(DVE, lambda e, pn_=pn_: e.tensor_tensor(out=pn_[0:tq, 0:128], in0=pn_[0:tq, 0:128], in1=EBn[j][:, :], op=ALU.mult), [pnb_, EBb], [pnb_])
                for h in range(8):
                    pk = 2 + h // 4
                    osl = P[pk][0:tq, (h % 4) * 65:(h % 4) * 65 + 65]
                    for kt in range(4):
                        sc.op(PE, lambda e, pt_=pt_, h=h, kt=kt, osl=osl: e.matmul(osl, pt_[:, (h * 4 + kt) * tq:(h * 4 + kt + 1) * tq], sve[:, kt, h, :], start=(kt == 0), stop=False), [ptb_, SVextb], [Pb[pk]])
                    sc.op(PE, lambda e, pn_=pn_, h=h, osl=osl: e.matmul(osl, pn_[0:tq, h * tq:(h + 1) * tq], SVnA[:, h, :], start=False, stop=True), [pnb_, SVnAb], [Pb[pk]])
                for half in range(2):
                    pk = 2 + half
                    ov = P[pk][0:tq, 0:260].rearrange("p (h c) -> p h c", h=4)
                    sc.op(DVE, lambda e, ov=ov, half=half: e.reciprocal(out=rl[0:tq, 8 + half * 4:12 + half * 4], in_=ov[:, :, 64]), [Pb[pk]], [rlb])
                    sc.op(DVE, lambda e, ov=ov, half=half: e.tensor_tensor(out=cat_s[:, half * 256:(half + 1) * 256].rearrange("p (h d) -> p h d", h=4), in0=ov[:, :, 0:64],
                                                                         in1=rl[0:tq, 8 + half * 4:12 + half * 4].unsqueeze(2).to_broadcast([tq, 4, 64]), op=ALU.mult), [Pb[pk], rlb], [cat_sb])
                nkt = PAST // 128
                for h in range(4):
                    sc.dma(POOL, SKraw[:, :].rearrange("p (k f) -> p k f", k=nkt), cbk[j, bb, :, h * 128:(h + 1) * 128].rearrange("(k p) f -> p k f", p=128), writes=[SKrawb])
                    sc.dma(POOL, SVraw[:, :].rearrange("p (k f) -> p k f", k=nkt), cbv[j, bb, :, h * 128:(h + 1) * 128].rearrange("(k p) f -> p k f", p=128), writes=[SVrawb])
                    skr2 = SKraw[:, :].rearrange("p (k f) -> p k f", k=nkt)
                    for g in range(nkt // 4):
                        pt, ptb = npt()
                        for k4 in range(4):
                            kt = g * 4 + k4
                            sc.op(PE, lambda e, pt=pt, k4=k4, kt=kt: e.transpose(pt[:, k4 * 128:(k4 + 1) * 128], skr2[:, kt, :], ident[:, :]), [SKrawb, constb], [ptb])
                        evac(SKT[:, g * 512:(g + 1) * 512], pt[:, 0:512], [ptb], [SKTb])
                    sve2 = SVext[:, 0:nkt * 129].rearrange("p (k c) -> p k c", k=nkt)
                    sc.op(DVE, lambda e, sve2=sve2: e.tensor_copy(out=sve2[:, :, 0:128], in_=SVraw[:, :].rearrange("p (k d) -> p k d", k=nkt)), [SVrawb], [SVextb])
                    sc.op(DVE, lambda e, sve2=sve2: e.memset(sve2[:, :, 128:129], 1.0), [SVextb], [SVextb])
                    Os = []
                    for m in range(2):
                        pS, pN, pO = (0, 1, 2) if m == 0 else (3, 4, 5)
                        for kt in range(nkt):
                            sc.op(PE, lambda e, pS=pS, kt=kt, m=m: e.matmul(P[pS][:, kt * tq:(kt + 1) * tq], SKT[m * 64:(m + 1) * 64, kt * 128:(kt + 1) * 128], QbT[m * 64:(m + 1) * 64, h, cs0:cs0 + tq], start=True, stop=True),
                                  [SKTb, QbTb], [Pb[pS]])
                        sc.op(PE, lambda e, pN=pN, m=m: e.matmul(P[pN][0:tq, 0:tq], KbTc[m * 64:(m + 1) * 64, h, cs0:cs0 + tq], QbT[m * 64:(m + 1) * 64, h, cs0:cs0 + tq], start=True, stop=True),
                              [KbTcb, QbTb], [Pb[pN]])
                        pt_, ptb_ = ntb()
                        pn_, pnb_ = ntb()
                        sc.op(ACT, lambda e, pt_=pt_, pS=pS: e.activation(out=pt_[:, 0:nkt * tq], in_=P[pS][:, 0:nkt * tq], func=AF.Exp, scale=0.125), [Pb[pS]], [ptb_])
                        sc.op(ACT, lambda e, pn_=pn_, pN=pN: e.activation(out=pn_[0:tq, 0:tq], in_=P[pN][0:tq, 0:tq], func=AF.Exp, scale=0.125), [Pb[pN]], [pnb_])
                        osl = P[pO][0:tq, 0:129]
                        for kt in range(nkt):
                            sc.op(PE, lambda e, pt_=pt_, kt=kt, osl=osl, sve2=sve2: e.matmul(osl, pt_[:, kt * tq:(kt + 1) * tq], sve2[:, kt, :], start=(kt == 0), stop=False), [ptb_, SVextb], [Pb[pO]])
                        sc.op(PE, lambda e, pn_=pn_, osl=osl: e.matmul(osl, pn_[0:tq, 0:tq], SVnB[:, h, :], start=False, stop=True), [pnb_, SVnBb], [Pb[pO]])
                        Os.append((P[pO][0:tq, 0:129].rearrange("p (q c) -> p q c", q=1), Pb[pO]))
                    diff_combine(tq, Os[0][0], Os[0][1], Os[1][0], Os[1][1], 1, j,
                                 lambda q, h=h: cat_s[:, 512 + h * 128:512 + (h + 1) * 128], lambda q: cat_sb)
                pt, ptb = npt()
                for kc in range(8):
                    sc.op(PE, lambda e, pt=pt, kc=kc: e.transpose(pt[:, kc * tq:(kc + 1) * tq], cat_s[:, kc * 128:(kc + 1) * 128], ident[:tq, :tq]), [cat_sb, constb], [ptb])
                evac(hT[:, :, cs0:cs0 + tq], pt[:, 0:8 * tq].rearrange("p (k t) -> p k t", k=8), [ptb], [hTb])

        tiles = [dict(kind="p", idx=t, nb=4, bp=128, ntok=TT, last=(t == NT - 1)) for t in range(NT)]
        tiles.append(dict(kind="s", idx=0, nb=1, bp=NS, ntok=NS, last=False))
        for tile in tiles:
            nb, bp = tile["nb"], tile["bp"]
            isS = tile["kind"] == "s"
            for b in range(nb):
                src = xs[0:NS, :] if isS else xp[tile["idx"] * TT + b * 128:tile["idx"] * TT + (b + 1) * 128, :]
                sc.dma(SP, X[:bp, b, :], src, writes=[Xb[b]])
            for i in range(4):
                if i % 2 == 0:
                    if "a" in DBG:
                        attn(tile, i)
                else:
                    if "c" in DBG:
                        conv(tile, i)
                if "f" in DBG:
                    ffn(tile, i)
            sc.op(DVE, lambda e: e.memset(ss[:, 0:nb], 0.0), [], [ssb])
            for b in range(nb):
                sc.op(ACT, lambda e, b=b: e.activation(out=junk[:bp, :], in_=X[:bp, b, :], func=AF.Square, accum_out=ss[:bp, b:b + 1]), [Xb[b], ssb], [ssb])
            sc.op(ACT, lambda e: e.activation(out=rstd[:bp, 0:nb], in_=ss[:bp, 0:nb], func=AF.Ln, scale=1.0 / D, bias=EPS), [ssb], [rstdb])
            sc.op(ACT, lambda e: e.activation(out=rstd[:bp, 0:nb], in_=rstd[:bp, 0:nb], func=AF.Exp, scale=-0.5), [rstdb], [rstdb])
            for b in range(nb):
                sc.op(DVE, lambda e, b=b: e.scalar_tensor_tensor(out=X[:bp, b, :], in0=X[:bp, b, :], scalar=rstd[:bp, b:b + 1], in1=gfin_s[:bp, :], op0=ALU.mult, op1=ALU.mult), [Xb[b], rstdb, constb], [Xb[b]])
                dst = ys[0:NS, :] if isS else yp[tile["idx"] * TT + b * 128:tile["idx"] * TT + (b + 1) * 128, :]
                sc.dma(SP, dst, X[:bp, b, :], reads=[Xb[b]])
        sc.finish()
        sc.emit()
    return nc


def _host_consts(S, NSB, TQ, PAST, rel_bias):
    half = 32
    inv = (10000.0 ** (-np.arange(half, dtype=np.float32) / half)).astype(np.float32)

    def rot_table(pos):
        ang = pos.astype(np.float32)[:, None] * inv[None, :]
        c, s = np.cos(ang).astype(np.float32), np.sin(ang).astype(np.float32)
        return np.concatenate([c, c, -s, s], axis=1).astype(np.float32)

    rotP = rot_table(np.arange(S))
    rotS = rot_table(PAST + (np.arange(NSB * TQ) % TQ))
    p = np.arange(128)[:, None, None]
    jj = np.array([3, 4])[None, :, None]
    ii = np.arange(128)[None, None, :]
    idxP = np.clip((4 - jj) * 128 + ii - p, -128, 128) + 128
    biasP = rel_bias[:, :, idxP]
    biasP = np.ascontiguousarray(biasP.transpose(0, 2, 1, 3, 4)).reshape(2, 128, 8 * 256)
    biasC = np.ascontiguousarray(np.broadcast_to(rel_bias[:, :, 256].reshape(1, 16), (128, 16)))
    kt = np.arange(4)[None, :, None]
    iq = np.arange(TQ)[None, None, :]
    idxS = np.clip(512 + iq - (kt * 128 + p), -128, 128) + 128
    biasS = rel_bias[:, :, idxS]
    biasS = np.ascontiguousarray(biasS.transpose(0, 2, 1, 3, 4)).reshape(2, 128, 8 * 4 * TQ)
    ik = np.arange(TQ)[:, None]
    idxN = np.clip(np.arange(TQ)[None, :] - ik, -128, 128) + 128
    biasSn = rel_bias[:, :, idxN]
    biasSn = np.ascontiguousarray(biasSn.transpose(0, 2, 1, 3)).reshape(2, TQ, 8 * TQ)
    return rotP, rotS, biasP.astype(np.float32), biasC.astype(np.float32), biasS.astype(np.float32), biasSn.astype(np.float32)


def make_in_maps(inp, n_cores, S, NSB, TQ=16, PAST=2048):
    f = lambda a: np.ascontiguousarray(np.asarray(a, dtype=np.float32))
    rotP, rotS, biasP, biasC, biasS, biasSn = _host_consts(S, NSB, TQ, PAST, f(inp["rel_bias"]))
    norm_all = np.concatenate([f(inp["norm_mix"]), f(inp["norm_ffn"]), f(inp["norm_final"])[None]], axis=0)
    normT = np.ascontiguousarray(norm_all.reshape(9, 8, 128).transpose(2, 0, 1)).reshape(128, 72)
    gfin = np.ascontiguousarray(np.broadcast_to(f(inp["norm_final"])[None, :], (128, D)))
    convw = np.ascontiguousarray(f(inp["conv_w"]).reshape(2, 3, 8, 128).transpose(3, 0, 1, 2)).reshape(128, 48)
    lam = np.stack([f(inp["lambda_q1"]), f(inp["lambda_k1"]), f(inp["lambda_q2"]), f(inp["lambda_k2"])], axis=1)
    lam_bc = np.ascontiguousarray(np.broadcast_to(lam.reshape(1, 512), (128, 512)))
    subg_bc = np.ascontiguousarray(np.broadcast_to(f(inp["subln_g"]).reshape(1, 256), (128, 256)))
    shared = dict(normT=normT, gfin=gfin, w_ain=f(inp["w_attn_in"]), w_aout=f(inp["w_attn_out"]), w_cin=f(inp["w_conv_in"]),
                  w_cout=f(inp["w_conv_out"]), w_g=f(inp["w_ffn_gate"]), w_u=f(inp["w_ffn_up"]), w_d=f(inp["w_ffn_down"]),
                  convw=convw, lam_bc=lam_bc, subg_bc=subg_bc, biasP=biasP, biasC=biasC, biasS=biasS, biasSn=biasSn,
                  rotP=rotP, rotS=rotS, ident=np.eye(128, dtype=np.float32))
    maps = []
    xpr, xsm = f(inp["x_prompt"]), f(inp["x_sample"])
    cak, cav, cbk, cbv, scv = f(inp["cache_a_k"]), f(inp["cache_a_v"]), f(inp["cache_b_k"]), f(inp["cache_b_v"]), f(inp["state_conv"])
    for c in range(n_cores):
        sl = slice(c * NSB, (c + 1) * NSB)
        m = dict(shared)
        m["xp"] = np.ascontiguousarray(xpr[c])
        m["xs"] = np.ascontiguousarray(xsm[sl].reshape(NSB * TQ, D))
        m["cak"] = np.ascontiguousarray(cak[:, sl].reshape(2, NSB, cak.shape[2], 512))
        m["cav"] = np.ascontiguousarray(cav[:, sl].reshape(2, NSB, cav.shape[2], 512))
        m["cbk"] = np.ascontiguousarray(cbk[:, sl].reshape(2, NSB, cbk.shape[2], 512))
        m["cbv"] = np.ascontiguousarray(cbv[:, sl].reshape(2, NSB, cbv.shape[2], 512))
        m["sconv"] = np.ascontiguousarray(scv[:, sl])
        maps.append(m)
    return maps


def assemble(results, n_cores, S, NSB, TQ=16):
    r = results
    cat = lambda k, ax: np.concatenate([np.expand_dims(x[k], ax) if ax is not None else x[k] for x in r], axis=ax if ax is not None else 0)
    y_prompt = np.stack([x["yp"] for x in r], 0)
    y_sample = np.concatenate([x["ys"].reshape(NSB, TQ, D) for x in r], 0)
    pak = np.stack([x["pak"].reshape(2, 512, 8, 64) for x in r], 1)
    pav = np.stack([x["pav"].reshape(2, 512, 8, 64) for x in r], 1)
    pbk = np.stack([x["pbk"].reshape(2, S, 4, 2, 64) for x in r], 1)
    pbv = np.stack([x["pbv"].reshape(2, S, 4, 128) for x in r], 1)
    pcs = np.stack([x["pcs"] for x in r], 1)
    sak = np.concatenate([x["sak"].reshape(2, NSB, TQ, 8, 64) for x in r], 1)
    sav = np.concatenate([x["sav"].reshape(2, NSB, TQ, 8, 64) for x in r], 1)
    sbk = np.concatenate([x["sbk"].reshape(2, NSB, TQ, 4, 2, 64) for x in r], 1)
    sbv = np.concatenate([x["sbv"].reshape(2, NSB, TQ, 4, 128) for x in r], 1)
    scs = np.concatenate([x["scs"] for x in r], 1)
    return (y_prompt, y_sample, pak, pav, pbk, pbv, pcs, sak, sav, sbk, sbv, scs)


def kernel(**inputs):
    n = 8
    S = int(np.asarray(inputs["x_prompt"]).shape[1])
    NSB = int(np.asarray(inputs["x_sample"]).shape[0]) // n
    nc = build(S, NSB)
    maps = make_in_maps(inputs, n, S, NSB)
    res = run_bass_kernel_spmd(nc, maps, core_ids=list(range(n)))
    return assemble(res.results, n, S, NSB)
```
